# Optimizing a Trainium2 kernel written in Bass

```python
import jax, jax.numpy as jnp
from jax import lax
import numpy as np

D_MODEL = 1024
BATCH = 4
SEQ = 8192
DEPTH = 4

N_MIXERS = 3
N_MLA = (DEPTH + 2) // 3
N_CONV = (DEPTH + 1) // 3
N_HGRN = DEPTH // 3
PLE_DIM = 256
D_FF = ((8 * D_MODEL // 3 + 255) // 256) * 256
MLA_HEADS = 16
MLA_NOPE = 64
MLA_ROPE = 32
MLA_V = 64
MLA_Q_LORA = 384
MLA_KV_LORA = 256
ROPE_BASE = 10000.0
Q_BLOCK = 128
CONV_CH = D_MODEL
CONV_WIDTH = 31
HGRN_HEADS = 8
HGRN_HEAD_DIM = D_MODEL // HGRN_HEADS
HGRN_DIM = HGRN_HEADS * HGRN_HEAD_DIM
CHUNK = 64
EPS = 1e-6

kernel_name = 'hybrid_mla_conformer_hgrn2_trunk'


def _rmsnorm(x, g):
    xf = x.astype(jnp.float32)
    y = xf * lax.rsqrt(jnp.mean(xf * xf, axis=-1, keepdims=True) + EPS)
    return (y * g.astype(jnp.float32)).astype(x.dtype)


def _layernorm(x, g, b):
    xf = x.astype(jnp.float32)
    mu = jnp.mean(xf, axis=-1, keepdims=True)
    var = jnp.mean(jnp.square(xf - mu), axis=-1, keepdims=True)
    y = (xf - mu) * lax.rsqrt(var + EPS)
    return (y * g.astype(jnp.float32) + b.astype(jnp.float32)).astype(x.dtype)


def _rope_tables(positions, dtype):
    inv_freq = 1.0 / (ROPE_BASE ** (jnp.arange(0, MLA_ROPE, 2, dtype=jnp.float32) / MLA_ROPE))
    ang = positions.astype(jnp.float32)[..., None] * inv_freq
    return jnp.cos(ang).astype(dtype), jnp.sin(ang).astype(dtype)


def _rope(x, cos, sin):
    x1, x2 = jnp.split(x, 2, axis=-1)
    return jnp.concatenate([x1 * cos - x2 * sin, x2 * cos + x1 * sin], axis=-1)


def _causal_attention(q, k, v, scale):
    B, S, H, Dk = q.shape
    Dv = v.shape[-1]
    nb = S // Q_BLOCK
    qb = q.reshape(B, nb, Q_BLOCK, H, Dk).transpose(1, 0, 3, 2, 4)
    key_pos = jnp.arange(S)

    def one_block(args):
        qi, bi = args
        s = jnp.einsum('bhqd,bkhd->bhqk', qi, k).astype(jnp.float32) * scale
        q_pos = bi * Q_BLOCK + jnp.arange(Q_BLOCK)
        s = jnp.where(key_pos[None, :] <= q_pos[:, None], s, -jnp.inf)
        pr = jax.nn.softmax(s, axis=-1).astype(v.dtype)
        return jnp.einsum('bhqk,bkhd->bqhd', pr, v)

    out = lax.map(one_block, (qb, jnp.arange(nb)))
    return out.transpose(1, 0, 2, 3, 4).reshape(B, S, H, Dv)


def _mla(u, positions, w_in, q_norm_g, w_uq, kv_norm_g, w_ukv, w_out):
    B, S, _ = u.shape
    c = u @ w_in
    c_q, c_kv, k_r = jnp.split(c, [MLA_Q_LORA, MLA_Q_LORA + MLA_KV_LORA], axis=-1)
    q = (_rmsnorm(c_q, q_norm_g) @ w_uq).reshape(B, S, MLA_HEADS, MLA_NOPE + MLA_ROPE)
    kv = (_rmsnorm(c_kv, kv_norm_g) @ w_ukv).reshape(B, S, MLA_HEADS, MLA_NOPE + MLA_V)
    q_nope, q_rope = jnp.split(q, [MLA_NOPE], axis=-1)
    k_nope, v = jnp.split(kv, [MLA_NOPE], axis=-1)
    cos, sin = _rope_tables(positions, u.dtype)
    q_rope = _rope(q_rope, cos[:, :, None, :], sin[:, :, None, :])
    k_rope = _rope(k_r, cos, sin)
    q = jnp.concatenate([q_nope, q_rope], axis=-1)
    k = jnp.concatenate([k_nope, jnp.broadcast_to(k_rope[:, :, None, :], (B, S, MLA_HEADS, MLA_ROPE))], axis=-1)
    o = _causal_attention(q, k, v, (MLA_NOPE + MLA_ROPE) ** -0.5)
    return o.reshape(B, S, MLA_HEADS * MLA_V) @ w_out


def _conformer_conv(u, w_pw1, b_pw1, w_dw, b_dw, ln_g, ln_b, w_pw2, b_pw2):
    a = u @ w_pw1 + b_pw1
    a = a[..., :CONV_CH] * jax.nn.sigmoid(a[..., CONV_CH:])
    a = lax.conv_general_dilated(a, w_dw[:, None, :].astype(a.dtype), window_strides=(1,),
                                 padding=[(CONV_WIDTH - 1, 0)],
                                 dimension_numbers=('NWC', 'WIO', 'NWC'),
                                 feature_group_count=CONV_CH) + b_dw
    a = jax.nn.silu(_layernorm(a, ln_g, ln_b))
    return a @ w_pw2 + b_pw2


def _gla_chunk_scan(q, k, v, log_f):
    B, S, H, K = q.shape
    V = v.shape[-1]
    n = S // CHUNK

    def to_chunks(t):
        return t.reshape(B, n, CHUNK, H, t.shape[-1]).transpose(1, 0, 3, 2, 4)

    qc, kc, vc = to_chunks(q), to_chunks(k), to_chunks(v)
    bc = jnp.cumsum(to_chunks(log_f), axis=3)
    causal = jnp.tril(jnp.ones((CHUNK, CHUNK), dtype=bool))

    def step(state, xs):
        qi, ki, vi, bi = xs
        diff = bi[:, :, :, None, :] - bi[:, :, None, :, :]
        decay = jnp.exp(jnp.where(causal[:, :, None], diff, -jnp.inf))
        attn = jnp.einsum('bhtsk,bhsk->bhts', qi[:, :, :, None, :] * decay, ki)
        o = jnp.einsum('bhts,bhsv->bhtv', attn, vi) + jnp.einsum('bhtk,bhkv->bhtv', qi * jnp.exp(bi), state)
        b_last = bi[:, :, -1:, :]
        new_state = jnp.exp(b_last[:, :, 0, :])[..., None] * state + \
            jnp.einsum('bhsk,bhsv->bhkv', ki * jnp.exp(b_last - bi), vi)
        return new_state, o

    s0 = jnp.zeros((B, H, K, V), jnp.float32)
    _, o = lax.scan(step, s0, (qc, kc, vc, bc))
    return o.transpose(1, 0, 3, 2, 4).reshape(B, S, H, V)


def _hgrn2(u, lb, w_in, norm_g, w_out):
    B, S, _ = u.shape
    proj = u @ w_in
    q, fz, i_in, g = jnp.split(proj, 4, axis=-1)
    lbf = lb.astype(jnp.float32)
    log_f = jnp.logaddexp(jnp.log(lbf), jnp.log1p(-lbf) + jax.nn.log_sigmoid(fz.astype(jnp.float32)))
    k = -jnp.expm1(log_f)
    hs = (B, S, HGRN_HEADS, HGRN_HEAD_DIM)
    o = _gla_chunk_scan(q.astype(jnp.float32).reshape(hs), k.reshape(hs),
                        i_in.astype(jnp.float32).reshape(hs), log_f.reshape(hs))
    o = _rmsnorm(o, norm_g.reshape(HGRN_HEADS, HGRN_HEAD_DIM)).reshape(B, S, HGRN_DIM).astype(u.dtype)
    o = o * jax.nn.sigmoid(g)
    return o @ w_out


def _swiglu(u, w_gu, w_down):
    gate, up = jnp.split(u @ w_gu, 2, axis=-1)
    return (jax.nn.silu(gate) * up) @ w_down


def setup_inputs(seed: int = 0) -> dict:
    key = jax.random.key(seed)
    ks = iter(jax.random.split(key, 40))

    def nrm(shape, scale):
        return jax.random.normal(next(ks), shape, jnp.float32) * scale

    def gain(shape):
        return 1.0 + nrm(shape, 0.02)

    D = D_MODEL
    out_scale = (2.0 * DEPTH) ** -0.5
    x = nrm((BATCH, SEQ, D), 1.0)
    p = nrm((DEPTH, BATCH, SEQ, PLE_DIM), 1.0)
    offsets = jax.random.randint(next(ks), (BATCH, 1), 0, 4096, dtype=jnp.int32)
    positions = offsets + jnp.arange(SEQ, dtype=jnp.int32)[None, :]
    return {
        'x': x, 'p': p, 'positions': positions,
        'norm1_g': gain((DEPTH, D)), 'norm2_g': gain((DEPTH, D)),
        'mla_w_in': nrm((N_MLA, D, MLA_Q_LORA + MLA_KV_LORA + MLA_ROPE), D ** -0.5),
        'mla_q_norm_g': gain((N_MLA, MLA_Q_LORA)),
        'mla_w_uq': nrm((N_MLA, MLA_Q_LORA, MLA_HEADS * (MLA_NOPE + MLA_ROPE)), MLA_Q_LORA ** -0.5),
        'mla_kv_norm_g': gain((N_MLA, MLA_KV_LORA)),
        'mla_w_ukv': nrm((N_MLA, MLA_KV_LORA, MLA_HEADS * (MLA_NOPE + MLA_V)), MLA_KV_LORA ** -0.5),
        'mla_w_out': nrm((N_MLA, MLA_HEADS * MLA_V, D), (MLA_HEADS * MLA_V) ** -0.5 * out_scale),
        'conv_w_pw1': nrm((N_CONV, D, 2 * CONV_CH), D ** -0.5),
        'conv_b_pw1': nrm((N_CONV, 2 * CONV_CH), 0.01),
        'conv_w_dw': nrm((N_CONV, CONV_WIDTH, CONV_CH), CONV_WIDTH ** -0.5),
        'conv_b_dw': nrm((N_CONV, CONV_CH), 0.01),
        'conv_ln_g': gain((N_CONV, CONV_CH)),
        'conv_ln_b': nrm((N_CONV, CONV_CH), 0.01),
        'conv_w_pw2': nrm((N_CONV, CONV_CH, D), CONV_CH ** -0.5 * out_scale),
        'conv_b_pw2': nrm((N_CONV, D), 0.01),
        'hgrn_w_in': nrm((N_HGRN, D, 4 * HGRN_DIM), D ** -0.5),
        'hgrn_lb_logits': nrm((DEPTH, HGRN_DIM), 1.0),
        'hgrn_norm_g': gain((N_HGRN, HGRN_DIM)),
        'hgrn_w_out': nrm((N_HGRN, HGRN_DIM, D), HGRN_DIM ** -0.5 * out_scale),
        'ffn_w_gu': nrm((DEPTH, D, 2 * D_FF), D ** -0.5),
        'ffn_w_down': nrm((DEPTH, D_FF, D), D_FF ** -0.5 * out_scale),
        'ple_w_proj': nrm((DEPTH, PLE_DIM, D), PLE_DIM ** -0.5 * out_scale),
        'ple_norm_g': gain((DEPTH, D)),
        'ple_w_gate': nrm((DEPTH, D, D), D ** -0.5),
        'final_norm_g': gain((D,)),
    }


def reference(x, p, positions, norm1_g, norm2_g,
              mla_w_in, mla_q_norm_g, mla_w_uq, mla_kv_norm_g, mla_w_ukv, mla_w_out,
              conv_w_pw1, conv_b_pw1, conv_w_dw, conv_b_dw, conv_ln_g, conv_ln_b, conv_w_pw2, conv_b_pw2,
              hgrn_w_in, hgrn_lb_logits, hgrn_norm_g, hgrn_w_out,
              ffn_w_gu, ffn_w_down, ple_w_proj, ple_norm_g, ple_w_gate, final_norm_g):
    lb_cum = jnp.cumsum(jax.nn.softmax(hgrn_lb_logits.astype(jnp.float32), axis=0), axis=0)
    lower_bounds = lb_cum - lb_cum[0:1]
    h = x
    for i in range(DEPTH):
        mixer = i % N_MIXERS
        j = i // N_MIXERS
        u = _rmsnorm(h, norm1_g[i])
        if mixer == 0:
            h = h + _mla(u, positions, mla_w_in[j], mla_q_norm_g[j], mla_w_uq[j],
                         mla_kv_norm_g[j], mla_w_ukv[j], mla_w_out[j])
        elif mixer == 1:
            h = h + _conformer_conv(u, conv_w_pw1[j], conv_b_pw1[j], conv_w_dw[j], conv_b_dw[j],
                                    conv_ln_g[j], conv_ln_b[j], conv_w_pw2[j], conv_b_pw2[j])
        else:
            h = h + _hgrn2(u, lower_bounds[i], hgrn_w_in[j], hgrn_norm_g[j], hgrn_w_out[j])
        h = h + _swiglu(_rmsnorm(h, norm2_g[i]), ffn_w_gu[i], ffn_w_down[i])
        gate = jax.nn.sigmoid(_rmsnorm(h, ple_norm_g[i]) @ ple_w_gate[i])
        h = h + (p[i] @ ple_w_proj[i]) * gate
    return _rmsnorm(h, final_norm_g)
```

```python
import contextlib
import numpy as np
import concourse.bass as bass
import concourse.mybir as mybir
from concourse.bass_utils import run_bass_kernel_spmd

F32 = mybir.dt.float32
BF16 = mybir.dt.bfloat16
I32 = mybir.dt.int32
AF = mybir.ActivationFunctionType
ALU = mybir.AluOpType

D = 1024
T = 4096
NT = 512
NTILES = T // NT
DFF = 2816
EPS = 1e-6
NCORES = 8
DEBUG = None
TWO_PI = float(2 * np.pi)


class R:
    __slots__ = ("w", "r")

    def __init__(self):
        self.w = []
        self.r = []


class Eng:
    def __init__(self, nc, eng, name, es):
        self.nc = nc
        self.eng = eng
        self.name = name
        self.sem = es.enter_context(nc.semaphore("s_" + name))
        self.cnt = 0
        self.seen = {}

    def wait(self, tok):
        if tok is None:
            return
        sem, val = tok
        key = id(sem)
        if self.seen.get(key, 0) >= val:
            return
        self.eng.wait_ge(sem, val)
        self.seen[key] = val

    def mark(self, ins):
        ins.then_inc(self.sem, 1)
        self.cnt += 1
        return (self.sem, self.cnt)


def begin(E, reads=(), writes=()):
    for x in reads:
        for t in x.w:
            E.wait(t)
    for x in writes:
        for t in x.w:
            E.wait(t)
        for t in x.r:
            E.wait(t)


def end(tok, reads=(), writes=(), add=()):
    for x in reads:
        x.r.append(tok)
    for x in writes:
        x.w = [tok]
        x.r = []
    for x in add:
        x.w.append(tok)


def op(E, fn, reads=(), writes=()):
    begin(E, reads, writes)
    ins = fn()
    tok = E.mark(ins)
    end(tok, reads, writes)
    return tok


class DmaSems:
    def __init__(self, nc, es, n, name):
        self.sems = [es.enter_context(nc.semaphore(f"{name}{i}")) for i in range(n)]
        self.vals = [0] * n
        self.i = 0

    def next(self):
        i = self.i
        self.i = (self.i + 1) % len(self.sems)
        return i


def dma(Q, pool, out, in_, reads=(), writes=(), add=()):
    begin(Q, reads, writes)
    i = pool.next()
    if pool.vals[i] > 0:
        Q.wait((pool.sems[i], pool.vals[i]))
    ins = Q.eng.dma_start(out=out, in_=in_)
    pool.vals[i] += 16
    ins.then_inc(pool.sems[i], 16)
    tok = (pool.sems[i], pool.vals[i])
    end(tok, reads, writes, add)
    return tok


def fm(v):
    v = np.asarray(v, np.float32)
    return np.ascontiguousarray(v.reshape(-1, 128).T)


class VecPack:
    def __init__(self):
        self.cols = []
        self.off = {}
        self.n = 0

    def add(self, name, arr):
        arr = np.asarray(arr, np.float32)
        assert arr.shape[0] == 128
        self.off[name] = self.n
        self.cols.append(arr)
        self.n += arr.shape[1]

    def build(self):
        return np.ascontiguousarray(np.concatenate(self.cols, axis=1))


class Ctx:
    pass


def build_program(layer, kind, voff, nvec, last, state_only=False):
    nc = bass.Bass("TRN2", target_bir_lowering=False)
    g = Ctx()
    g.nc = nc
    g.kind = kind
    g.voff = voff
    g.last = last

    def din(name, shape, dt=F32):
        return nc.dram_tensor(name, list(shape), dt, kind="ExternalInput").ap()

    def dout(name, shape, dt=F32):
        return nc.dram_tensor(name, list(shape), dt, kind="ExternalOutput").ap()

    def dint(name, shape, dt):
        return nc.dram_tensor(name, list(shape), dt, kind="Internal").ap()

    g.hT = din("hT", [D, T])
    g.hprevT = din("hprevT", [D, T])
    g.vecs_d = din("vecs", [128, nvec])
    g.valid_d = din("valid", [128, 1])
    if not state_only:
        g.pT = din("pT", [256, T])
        g.w_gu = din("w_gu", [D, 2 * DFF])
        g.w_down = din("w_down", [DFF, D])
        g.w_pproj = din("w_pproj", [256, D])
        g.w_pgate = din("w_pgate", [D, D])
        g.w_o = din("w_o", [D, D])
        g.outT = dout("outT", [D, T])
        if DEBUG:
            g.dbg = dout("dbg", [D, T])
    if kind == "mla":
        g.pos = din("pos", [1, 2 * T], I32)
        g.w_in = din("w_in", [D, 704])
        g.w_uq = din("w_uq", [384, 16 * 96])
        g.w_uqs = din("w_uqs", [384, 16 * 32])
        g.w_ukv = din("w_ukv", [256, 2048])
        g.masks_d = din("masks", [128, 4 * NT])
        dd = dout if DEBUG == "mla" else dint
        g.QT = dd("QT", [16, 96, T], BF16)
        g.KnT = dd("KnT", [D, 2 * T], BF16)
        g.KrT = dd("KrT", [32, 2 * T], BF16)
        g.Vd = dd("Vd", [2 * T, D], BF16)
        g.oT = dd("oT", [D, T], BF16)
        g.tab = dd("tab", [2, 32, 2 * T], F32)
    if kind == "conv":
        g.w_pw1 = din("w_pw1", [D, 2 * D])
    if kind == "hgrn":
        g.w_hin = din("w_hin", [D, 4 * D])
        g.hmask_d = din("hmask", [128, 128])
        g.reset_d = din("reset", [128, NT])
        g.ident_d = din("ident", [128, 128])
        g.S0 = din("S0", [8, 128, 128])
        g.Sout = dout("Sout", [8, 128, 128])

    with contextlib.ExitStack() as es:
        g.es = es
        g.PE = Eng(nc, nc.tensor, "pe", es)
        g.ACT = Eng(nc, nc.scalar, "act", es)
        g.DVE = Eng(nc, nc.vector, "dve", es)
        g.POOL = Eng(nc, nc.gpsimd, "pool", es)
        g.SP = Eng(nc, nc.sync, "sp", es)
        g.dsp = DmaSems(nc, es, 24, "dsp")
        g.dpl = DmaSems(nc, es, 24, "dpl")

        def sb(name, shape, dt):
            return es.enter_context(nc.sbuf_tensor(name, list(shape), dt))

        g.sb = sb
        g.banks = [es.enter_context(nc.psum_tensor(f"ps{i}", [128, 512], F32)) for i in range(8)]
        g.bankR = [R() for _ in range(8)]
        g.bank_i = 0
        g.bank_l = 0

        g.vecs = sb("vecs_sb", [128, nvec], F32)
        g.vecsR = R()
        dma(g.SP, g.dsp, g.vecs[:], g.vecs_d[:, :], writes=[g.vecsR])
        g.valid = sb("valid_sb", [128, 1], F32)
        g.validR = R()
        dma(g.SP, g.dsp, g.valid[:], g.valid_d[:, :], writes=[g.validR])
        g.ones = sb("ones_sb", [128, 128], BF16)
        g.onesR = R()
        op(g.POOL, lambda: nc.gpsimd.memset(g.ones[:], 1.0), writes=[g.onesR])
        g.epsc = sb("eps_sb", [128, 1], F32)
        g.epsR = R()
        op(g.POOL, lambda: nc.gpsimd.memset(g.epsc[:], EPS), writes=[g.epsR])

        g.NSLOT = 4
        g.wslots = [sb(f"wslot{i}", [128, 5632], BF16) for i in range(g.NSLOT)]
        g.wslotR = [R() for _ in range(g.NSLOT)]
        g.wplan = []
        g.wissued = 0
        g.wused = 0

        if kind == "mla":
            mla_layer(g)
        elif kind == "conv":
            conv_layer(g)
        else:
            hgrn_layer(g, state_only)

        for i in range(len(g.dsp.sems)):
            if g.dsp.vals[i] > 0:
                g.SP.wait((g.dsp.sems[i], g.dsp.vals[i]))
        for i in range(len(g.dpl.sems)):
            if g.dpl.vals[i] > 0:
                g.POOL.wait((g.dpl.sems[i], g.dpl.vals[i]))
    return nc


def barrier(g):
    engs = [g.PE, g.ACT, g.DVE, g.POOL, g.SP]
    toks = [(E.sem, E.cnt) for E in engs if E.cnt > 0]
    for pool in (g.dsp, g.dpl):
        for i in range(len(pool.sems)):
            if pool.vals[i] > 0:
                toks.append((pool.sems[i], pool.vals[i]))
    for E in engs:
        for t in toks:
            E.wait(t)


def col(g, name, j=0):
    o = g.voff[name] + j
    return g.vecs[:, o:o + 1]


def newbank(g):
    i = g.bank_i
    g.bank_i = (g.bank_i + 1) % 6
    return g.banks[i], g.bankR[i]


def newbank_long(g):
    i = 6 + g.bank_l
    g.bank_l = (g.bank_l + 1) % 2
    return g.banks[i], g.bankR[i]


def wplan_extend(g, specs):
    g.wplan.extend(specs)


def _wissue(g):
    n = g.wissued
    if n >= len(g.wplan):
        return
    src, kc, ncol = g.wplan[n]
    slot = n % g.NSLOT
    dst = g.wslots[slot][:, 0:kc * ncol].rearrange("p (k n) -> p k n", k=kc)
    dma(g.POOL, g.dpl, dst, src, writes=[g.wslotR[slot]])
    g.wissued += 1


def wget(g):
    n = g.wused
    lim = min(n + g.NSLOT - 2, len(g.wplan) - 1)
    while g.wissued <= lim:
        _wissue(g)
    src, kc, ncol = g.wplan[n]
    slot = n % g.NSLOT
    g.wused += 1
    view = g.wslots[slot][:, 0:kc * ncol].rearrange("p (k n) -> p k n", k=kc)
    return view, g.wslotR[slot]


def wspec(w_ap, kc, c0, ncol):
    return (w_ap.rearrange("(k p) n -> p k n", p=128)[:, :, c0:c0 + ncol], kc, ncol)


def rstd_from_sumsq(g, ss_bank, ss_R, n, rs, rsR, ncols=NT):
    nc = g.nc
    op(g.ACT, lambda: nc.scalar.activation(out=rs[:, 0:ncols], in_=ss_bank[:, 0:ncols], func=AF.Ln,
                                           bias=g.epsc[:, 0:1], scale=1.0 / n),
       reads=[ss_R, g.epsR], writes=[rsR])
    op(g.ACT, lambda: nc.scalar.activation(out=rs[:, 0:ncols], in_=rs[:, 0:ncols], func=AF.Exp, scale=-0.5),
       reads=[rsR], writes=[rsR])


def rmsnorm_tile(g, hT, hR, gname, u, uR, S, ncols=NT):
    nc = g.nc
    sq, sqR, rs, rsR = S.sq, S.sqR, S.rs, S.rsR
    op(g.ACT, lambda: nc.scalar.activation(out=sq[:, :, 0:ncols], in_=hT[:, :, 0:ncols], func=AF.Square),
       reads=hR, writes=[sqR])
    bank, bR = newbank(g)
    begin(g.PE, reads=[sqR, g.onesR], writes=[bR])
    for c in range(8):
        ins = nc.tensor.matmul(bank[:, 0:ncols], lhsT=g.ones[:], rhs=sq[:, c, 0:ncols], start=(c == 0), stop=(c == 7))
    tok = g.PE.mark(ins)
    end(tok, reads=[sqR, g.onesR], writes=[bR])
    rstd_from_sumsq(g, bank, bR, D, rs, rsR, ncols)
    for c in range(8):
        op(g.DVE, lambda c=c: nc.vector.scalar_tensor_tensor(out=u[:, c, 0:ncols], in0=hT[:, c, 0:ncols], scalar=col(g, gname, c),
                                                            in1=rs[:, 0:ncols], op0=ALU.mult, op1=ALU.mult),
           reads=[hR[c], rsR, g.vecsR], writes=[uR[c]])


def proj_chunk(g, wv, wR, c0, xin, xR, kc, mrows=128, ncols=NT, out_ap=None, bank=None, bR=None):
    nc = g.nc
    if bank is None:
        bank, bR = newbank(g)
    out = bank[0:mrows, 0:ncols] if out_ap is None else out_ap
    begin(g.PE, reads=[wR] + list(xR), writes=[bR])
    for k in range(kc):
        ins = nc.tensor.matmul(out, lhsT=wv[:, k, c0:c0 + mrows], rhs=xin[:, k, 0:ncols], start=(k == 0), stop=(k == kc - 1))
    tok = g.PE.mark(ins)
    end(tok, reads=[wR] + list(xR), writes=[bR])
    return bank, bR


class TailBufs:
    pass


def alloc_tail(g):
    sb = g.sb
    S = TailBufs()
    S.hT = [sb(f"hT{i}", [128, 8, NT], F32) for i in range(2)]
    S.hR = [[R() for _ in range(8)] for _ in range(2)]
    S.pTb = [sb(f"pTb{i}", [128, 2, NT], BF16) for i in range(2)]
    S.pR = [R() for _ in range(2)]
    S.sq = sb("sq", [128, 8, NT], BF16)
    S.sqR = R()
    S.rs = sb("rs", [128, NT], F32)
    S.rsR = R()
    S.u = sb("u", [128, 8, NT], BF16)
    S.uR = [R() for _ in range(8)]
    S.act = sb("act", [128, 22, NT], BF16)
    S.actR = [R() for _ in range(22)]
    S.z = sb("z", [128, 8, NT], BF16)
    S.zR = [R() for _ in range(8)]
    S.tmp = [sb(f"tmp{i}", [128, NT], F32) for i in range(3)]
    S.tmpR = [R() for _ in range(3)]
    S.tmp_i = 0
    return S


def newtmp(S):
    i = S.tmp_i
    S.tmp_i = (S.tmp_i + 1) % len(S.tmp)
    return S.tmp[i], S.tmpR[i]


def tail_specs(g):
    sp = []
    for c0 in (0, 512):
        sp.append(wspec(g.w_o, 8, c0, 512))
    for b in range(6):
        n = 512 if b < 5 else 256
        sp.append(wspec(g.w_gu, 8, b * 512, n))
        sp.append(wspec(g.w_gu, 8, DFF + b * 512, n))
    for c0 in range(0, D, 256):
        sp.append(wspec(g.w_down, 22, c0, 256))
    for c0 in (0, 512):
        sp.append(wspec(g.w_pgate, 8, c0, 512))
    sp.append(wspec(g.w_pproj, 2, 0, 1024))
    return sp


def tail_tile(g, S, t, hT, hR, obias=None):
    nc = g.nc
    for half in range(2):
        wv, wR = wget(g)
        for cc in range(4):
            c = half * 4 + cc
            bank, bR = proj_chunk(g, wv, wR, cc * 128, S.z, S.zR, 8)
            if obias is None:
                op(g.DVE, lambda c=c, bank=bank: nc.vector.tensor_tensor(out=hT[:, c, :], in0=bank[:, :], in1=hT[:, c, :], op=ALU.add),
                   reads=[bR, hR[c]], writes=[hR[c]])
            else:
                op(g.DVE, lambda c=c, bank=bank: nc.vector.scalar_tensor_tensor(out=hT[:, c, :], in0=bank[:, :], scalar=col(g, obias, c),
                                                                            in1=hT[:, c, :], op0=ALU.add, op1=ALU.add),
                   reads=[bR, hR[c], g.vecsR], writes=[hR[c]])
    dbg_dump(g, "hm", t, hT[:, :, :], hR)
    rmsnorm_tile(g, hT, hR, "norm2_g", S.u, S.uR, S)
    dbg_dump(g, "u2", t, S.u[:, :, :], S.uR)
    for b in range(6):
        nj = 4 if b < 5 else 2
        wg, wgR = wget(g)
        wu, wuR = wget(g)
        for jj in range(nj):
            j = b * 4 + jj
            gb, gR = proj_chunk(g, wg, wgR, jj * 128, S.u, S.uR, 8)
            ub, uR_ = proj_chunk(g, wu, wuR, jj * 128, S.u, S.uR, 8)
            tmp, tR = newtmp(S)
            op(g.ACT, lambda gb=gb, tmp=tmp: nc.scalar.activation(out=tmp[:], in_=gb[:, :], func=AF.Silu),
               reads=[gR], writes=[tR])
            op(g.DVE, lambda ub=ub, tmp=tmp, j=j: nc.vector.tensor_tensor(out=S.act[:, j, :], in0=ub[:, :], in1=tmp[:], op=ALU.mult),
               reads=[uR_, tR], writes=[S.actR[j]])
    for q4 in range(4):
        wv, wR = wget(g)
        for cc in range(2):
            c = q4 * 2 + cc
            bank, bR = proj_chunk(g, wv, wR, cc * 128, S.act, S.actR, 22)
            op(g.DVE, lambda c=c, bank=bank: nc.vector.tensor_tensor(out=hT[:, c, :], in0=bank[:, :], in1=hT[:, c, :], op=ALU.add),
               reads=[bR, hR[c]], writes=[hR[c]])
    dbg_dump(g, "h2", t, hT[:, :, :], hR)
    rmsnorm_tile(g, hT, hR, "ple_norm_g", S.u, S.uR, S)
    for half in range(2):
        wv, wR = wget(g)
        for cc in range(4):
            c = half * 4 + cc
            bank, bR = proj_chunk(g, wv, wR, cc * 128, S.u, S.uR, 8)
            op(g.ACT, lambda c=c, bank=bank: nc.scalar.activation(out=S.sq[:, c, :], in_=bank[:, :], func=AF.Sigmoid),
               reads=[bR], writes=[S.sqR])
    wv, wR = wget(g)
    pb = S.pTb[t % 2]
    pR = S.pR[t % 2]
    for c in range(8):
        bank, bR = proj_chunk(g, wv, wR, c * 128, pb, [pR], 2)
        tmp, tR = newtmp(S)
        op(g.DVE, lambda c=c, bank=bank, tmp=tmp: nc.vector.tensor_tensor(out=tmp[:], in0=bank[:, :], in1=S.sq[:, c, :], op=ALU.mult),
           reads=[bR, S.sqR], writes=[tR])
        op(g.POOL, lambda c=c, tmp=tmp: nc.gpsimd.tensor_tensor(out=hT[:, c, :], in0=hT[:, c, :], in1=tmp[:], op=ALU.add),
           reads=[tR, hR[c]], writes=[hR[c]])
    if g.last:
        nc_ = nc
        op(g.ACT, lambda: nc_.scalar.activation(out=S.sq[:, :, :], in_=hT[:, :, :], func=AF.Square), reads=hR, writes=[S.sqR])
        bank, bR = newbank(g)
        begin(g.PE, reads=[S.sqR, g.onesR], writes=[bR])
        for c in range(8):
            ins = nc.tensor.matmul(bank[:, :], lhsT=g.ones[:], rhs=S.sq[:, c, :], start=(c == 0), stop=(c == 7))
        tok = g.PE.mark(ins)
        end(tok, reads=[S.sqR, g.onesR], writes=[bR])
        rstd_from_sumsq(g, bank, bR, D, S.rs, S.rsR)
        for c in range(8):
            op(g.DVE, lambda c=c: nc.vector.scalar_tensor_tensor(out=hT[:, c, :], in0=hT[:, c, :], scalar=col(g, "final_g", c),
                                                                in1=S.rs[:], op0=ALU.mult, op1=ALU.mult),
               reads=[hR[c], S.rsR, g.vecsR], writes=[hR[c]])
    dst = g.outT.rearrange("(c p) n -> p c n", p=128)[:, :, t * NT:(t + 1) * NT]
    dma(g.SP, g.dsp, dst, hT[:, :, :], reads=hR)


def dbg_dump(g, name, t, buf, regs):
    if DEBUG == name:
        dst = g.dbg.rearrange("(c p) n -> p c n", p=128)[:, :, t * NT:(t + 1) * NT]
        dma(g.POOL, g.dpl, dst, buf, reads=regs)


def load_h_tile(g, S, t, src=None, slot=None):
    src = g.hT if src is None else src
    slot = t % 2 if slot is None else slot
    s = src.rearrange("(c p) n -> p c n", p=128)[:, :, t * NT:(t + 1) * NT]
    dma(g.SP, g.dsp, S.hT[slot][:, :, :], s, writes=S.hR[slot])
    return S.hT[slot], S.hR[slot]


def load_p_tile(g, S, t):
    s = g.pT.rearrange("(c p) n -> p c n", p=128)[:, :, t * NT:(t + 1) * NT]
    dma(g.POOL, g.dpl, S.pTb[t % 2][:, :, :], s, writes=[S.pR[t % 2]])


def mla_layer(g):
    nc = g.nc
    sb = g.sb
    es_outer = g.es
    with contextlib.ExitStack() as es:
        def sbl(name, shape, dt):
            return es.enter_context(nc.sbuf_tensor(name, list(shape), dt))
        CH = 2048
        posi = sbl("posi", [32, CH], I32)
        posf = sbl("posf", [32, CH], F32)
        ang = sbl("ang", [32, CH], F32)
        kf = sbl("kf", [32, CH], F32)
        ki = sbl("ki", [32, CH], I32)
        y = sbl("yy", [32, CH], F32)
        ys = sbl("ys", [32, CH], F32)
        res = sbl("res", [32, CH], F32)
        rr = [R() for _ in range(8)]
        c1 = float(np.float32(6.28125))
        c2 = float(np.float32(2 * np.pi - 6.28125))
        c3 = float(2 * np.pi - 6.28125 - np.float64(np.float32(2 * np.pi - 6.28125)))
        for ch in range(2 * T // CH):
            dma(g.SP, g.dsp, posi[:], g.pos[0:1, ch * CH:(ch + 1) * CH].partition_broadcast(32), writes=[rr[0]])
            op(g.DVE, lambda: nc.vector.tensor_copy(out=posf[:], in_=posi[:]), reads=[rr[0]], writes=[rr[1]])
            op(g.DVE, lambda: nc.vector.tensor_scalar(out=ang[:], in0=posf[:], scalar1=g.vecs[0:32, g.voff["inv_freq"]:g.voff["inv_freq"] + 1],
                                                      scalar2=None, op0=ALU.mult),
               reads=[rr[1], g.vecsR], writes=[rr[2]])
            op(g.DVE, lambda: nc.vector.tensor_scalar(out=kf[:], in0=ang[:], scalar1=1.0 / TWO_PI, scalar2=None, op0=ALU.mult),
               reads=[rr[2]], writes=[rr[3]])
            op(g.DVE, lambda: nc.vector.tensor_copy(out=ki[:], in_=kf[:]), reads=[rr[3]], writes=[rr[4]])
            op(g.DVE, lambda: nc.vector.tensor_copy(out=kf[:], in_=ki[:]), reads=[rr[4]], writes=[rr[3]])
            op(g.DVE, lambda: nc.vector.scalar_tensor_tensor(out=y[:], in0=kf[:], scalar=-c1, in1=ang[:], op0=ALU.mult, op1=ALU.add),
               reads=[rr[2], rr[3]], writes=[rr[5]])
            op(g.DVE, lambda: nc.vector.scalar_tensor_tensor(out=y[:], in0=kf[:], scalar=-c2, in1=y[:], op0=ALU.mult, op1=ALU.add),
               reads=[rr[3], rr[5]], writes=[rr[5]])
            for which, shift in ((0, np.pi / 2), (1, 0.0)):
                op(g.DVE, lambda shift=shift: nc.vector.tensor_scalar(out=ys[:], in0=y[:], scalar1=float(np.pi - shift), scalar2=-TWO_PI,
                                                                     op0=ALU.is_gt, op1=ALU.mult),
                   reads=[rr[5]], writes=[rr[6]])
                op(g.DVE, lambda shift=shift: nc.vector.scalar_tensor_tensor(out=ys[:], in0=y[:], scalar=float(shift), in1=ys[:], op0=ALU.add, op1=ALU.add),
                   reads=[rr[5], rr[6]], writes=[rr[6]])
                op(g.ACT, lambda: nc.scalar.activation(out=res[:], in_=ys[:], func=AF.Sin), reads=[rr[6]], writes=[rr[7]])
                if which == 1:
                    op(g.DVE, lambda: nc.vector.tensor_scalar(out=res[0:16, :], in0=res[0:16, :], scalar1=-1.0, scalar2=None, op0=ALU.mult),
                       reads=[rr[7]], writes=[rr[7]])
                dma(g.SP, g.dsp, g.tab[which, :, ch * CH:(ch + 1) * CH], res[:], reads=[rr[7]])
        for i in range(len(g.dsp.sems)):
            if g.dsp.vals[i] > 0:
                g.SP.wait((g.dsp.sems[i], g.dsp.vals[i]))

    barrier(g)
    with contextlib.ExitStack() as es:
        def sbl(name, shape, dt):
            return es.enter_context(nc.sbuf_tensor(name, list(shape), dt))
        hTb = [sbl(f"a_hT{i}", [128, 8, NT], F32) for i in range(2)]
        hR = [[R() for _ in range(8)] for _ in range(2)]
        S = Ctx()
        S.sq = sbl("a_sq", [128, 8, NT], BF16); S.sqR = R()
        S.rs = sbl("a_rs", [128, NT], F32); S.rsR = R()
        u = sbl("a_u", [128, 8, NT], BF16); uR = [R() for _ in range(8)]
        w_in = sbl("a_win", [128, 8, 704], BF16); winR = R()
        w_uq = sbl("a_wuq", [128, 3, 1536], BF16); wuqR = R()
        w_uqs = sbl("a_wuqs", [128, 3, 512], BF16); wuqsR = R()
        w_ukv = sbl("a_wukv", [128, 2, 2048], BF16); wukvR = R()
        dma(g.POOL, g.dpl, w_in[:, :, :], g.w_in.rearrange("(k p) n -> p k n", p=128), writes=[winR])
        dma(g.POOL, g.dpl, w_uq[:, :, :], g.w_uq.rearrange("(k p) n -> p k n", p=128), writes=[wuqR])
        dma(g.POOL, g.dpl, w_uqs[:, :, :], g.w_uqs.rearrange("(k p) n -> p k n", p=128), writes=[wuqsR])
        dma(g.POOL, g.dpl, w_ukv[:, :, :], g.w_ukv.rearrange("(k p) n -> p k n", p=128), writes=[wukvR])
        cqn = sbl("a_cqn", [128, 3, NT], BF16); cqnR = [R() for _ in range(3)]
        ckvn = sbl("a_ckvn", [128, 2, NT], BF16); ckvnR = [R() for _ in range(2)]
        sq2 = sbl("a_sq2", [128, 3, NT], BF16); sq2R = R()
        rs2 = sbl("a_rs2", [128, NT], F32); rs2R = R()
        cs = [sbl(f"a_cos{i}", [96, NT], F32) for i in range(2)]
        sn = [sbl(f"a_sin{i}", [96, NT], F32) for i in range(2)]
        csR = [R() for _ in range(2)]
        snR = [R() for _ in range(2)]
        t1 = sbl("a_t1", [96, NT], F32); t1R = R()
        t2 = sbl("a_t2", [96, NT], F32); t2R = R()
        qt = [sbl(f"a_qt{i}", [96, NT], BF16) for i in range(2)]
        qtR = [R() for _ in range(2)]
        krt = sbl("a_krt", [96, NT], BF16); krtR = R()
        kn = [sbl(f"a_kn{i}", [128, NT], BF16) for i in range(2)]
        knR = [R() for _ in range(2)]
        vt = [sbl(f"a_vt{i}", [128, 1024], BF16) for i in range(2)]
        vtR = [R() for _ in range(2)]

        def load(i):
            src = g.hprevT if i < 8 else g.hT
            tt = i % 8
            s = src.rearrange("(c p) n -> p c n", p=128)[:, :, tt * NT:(tt + 1) * NT]
            dma(g.SP, g.dsp, hTb[i % 2][:, :, :], s, writes=hR[i % 2])
            dma(g.SP, g.dsp, cs[i % 2][64:96, :], g.tab[0, :, i * NT:(i + 1) * NT], writes=[csR[i % 2]])
            dma(g.SP, g.dsp, sn[i % 2][64:96, :], g.tab[1, :, i * NT:(i + 1) * NT], writes=[snR[i % 2]])

        load(0)
        for i in range(16):
            if i + 1 < 16:
                load(i + 1)
            own = i >= 8
            hT_, hR_ = hTb[i % 2], hR[i % 2]
            cs_, sn_, csR_, snR_ = cs[i % 2], sn[i % 2], csR[i % 2], snR[i % 2]
            rmsnorm_tile(g, hT_, hR_, "norm1_g", u, uR, S)
            def latent_norm(c_lo, nch, gname, n, outbuf, outR):
                banks = []
                for c in range(nch):
                    bank, bR = proj_chunk(g, w_in, winR, c_lo + c * 128, u, uR, 8)
                    banks.append((bank, bR))
                    op(g.ACT, lambda c=c, bank=bank: nc.scalar.activation(out=sq2[:, c, :], in_=bank[:, :], func=AF.Square),
                       reads=[bR], writes=[sq2R])
                sbank, sbR = newbank(g)
                begin(g.PE, reads=[sq2R, g.onesR], writes=[sbR])
                for c in range(nch):
                    ins = nc.tensor.matmul(sbank[:, :], lhsT=g.ones[:], rhs=sq2[:, c, :], start=(c == 0), stop=(c == nch - 1))
                tok = g.PE.mark(ins)
                end(tok, reads=[sq2R, g.onesR], writes=[sbR])
                rstd_from_sumsq(g, sbank, sbR, n, rs2, rs2R)
                for c in range(nch):
                    bank, bR = banks[c]
                    op(g.DVE, lambda c=c, bank=bank: nc.vector.scalar_tensor_tensor(out=outbuf[:, c, :], in0=bank[:, :], scalar=col(g, gname, c),
                                                                                in1=rs2[:], op0=ALU.mult, op1=ALU.mult),
                       reads=[bR, rs2R, g.vecsR], writes=[outR[c]])
            if own:
                latent_norm(0, 3, "q_norm_g", 384, cqn, cqnR)
            latent_norm(384, 2, "kv_norm_g", 256, ckvn, ckvnR)
            bank, bR = newbank(g)
            proj_chunk(g, w_in, winR, 640, u, uR, 8, mrows=32, out_ap=bank[64:96, :], bank=bank, bR=bR)
            bank2, bR2 = newbank(g)
            proj_chunk(g, w_in, winR, 672, u, uR, 8, mrows=32, out_ap=bank2[64:96, :], bank=bank2, bR=bR2)
            op(g.DVE, lambda: nc.vector.tensor_tensor(out=t1[64:96, :], in0=bank[64:96, :], in1=cs_[64:96, :], op=ALU.mult),
               reads=[bR, csR_], writes=[t1R])
            op(g.DVE, lambda: nc.vector.tensor_tensor(out=t2[64:96, :], in0=bank2[64:96, :], in1=sn_[64:96, :], op=ALU.mult),
               reads=[bR2, snR_], writes=[t2R])
            op(g.DVE, lambda: nc.vector.tensor_tensor(out=krt[64:96, :], in0=t1[64:96, :], in1=t2[64:96, :], op=ALU.add),
               reads=[t1R, t2R], writes=[krtR])
            dma(g.SP, g.dsp, g.KrT[:, i * NT:(i + 1) * NT], krt[64:96, :], reads=[krtR])
            for c in range(8):
                bank, bR = proj_chunk(g, w_ukv, wukvR, c * 128, ckvn, ckvnR, 2)
                kb, kR = kn[c % 2], knR[c % 2]
                op(g.ACT, lambda bank=bank, kb=kb: nc.scalar.copy(out=kb[:], in_=bank[:, :]), reads=[bR], writes=[kR])
                dma(g.SP, g.dsp, g.KnT[c * 128:(c + 1) * 128, i * NT:(i + 1) * NT], kb[:], reads=[kR])
            for s4 in range(4):
                vb, vR = vt[s4 % 2], vtR[s4 % 2]
                for hf in range(2):
                    bank, bR = newbank(g)
                    begin(g.PE, reads=ckvnR + [wukvR], writes=[bR])
                    for k in range(2):
                        ins = nc.tensor.matmul(bank[:, :], lhsT=ckvn[:, k, s4 * 128:(s4 + 1) * 128], rhs=w_ukv[:, k, 1024 + hf * 512:1024 + (hf + 1) * 512],
                                               start=(k == 0), stop=(k == 1))
                    tok = g.PE.mark(ins)
                    end(tok, reads=ckvnR + [wukvR], writes=[bR])
                    if own:
                        op(g.ACT, lambda bank=bank, vb=vb, hf=hf: nc.scalar.copy(out=vb[:, hf * 512:(hf + 1) * 512], in_=bank[:, :]),
                           reads=[bR], writes=[vR])
                    else:
                        op(g.ACT, lambda bank=bank, vb=vb, hf=hf: nc.scalar.activation(out=vb[:, hf * 512:(hf + 1) * 512], in_=bank[:, :],
                                                                                     func=AF.Copy, scale=g.valid[:, 0:1]),
                           reads=[bR, g.validR], writes=[vR])
                dma(g.SP, g.dsp, g.Vd[i * NT + s4 * 128:i * NT + (s4 + 1) * 128, :], vb[:], reads=[vR])
            if own:
                tt = i - 8
                for h in range(16):
                    bank, bR = proj_chunk(g, w_uq, wuqR, h * 96, cqn, cqnR, 3, mrows=96)
                    bank2, bR2 = newbank(g)
                    proj_chunk(g, w_uqs, wuqsR, h * 32, cqn, cqnR, 3, mrows=32, out_ap=bank2[64:96, :], bank=bank2, bR=bR2)
                    qb, qR = qt[h % 2], qtR[h % 2]
                    op(g.ACT, lambda bank=bank, qb=qb: nc.scalar.copy(out=qb[0:64, :], in_=bank[0:64, :]), reads=[bR], writes=[qR])
                    op(g.DVE, lambda bank=bank: nc.vector.tensor_tensor(out=t1[64:96, :], in0=bank[64:96, :], in1=cs_[64:96, :], op=ALU.mult),
                       reads=[bR, csR_], writes=[t1R])
                    op(g.DVE, lambda bank2=bank2: nc.vector.tensor_tensor(out=t2[64:96, :], in0=bank2[64:96, :], in1=sn_[64:96, :], op=ALU.mult),
                       reads=[bR2, snR_], writes=[t2R])
                    op(g.DVE, lambda qb=qb: nc.vector.tensor_tensor(out=qb[64:96, :], in0=t1[64:96, :], in1=t2[64:96, :], op=ALU.add),
                       reads=[t1R, t2R, qR], writes=[qR])
                    dma(g.SP, g.dsp, g.QT[h, :, tt * NT:(tt + 1) * NT], qb[:], reads=[qR])
        for i in range(len(g.dsp.sems)):
            if g.dsp.vals[i] > 0:
                g.SP.wait((g.dsp.sems[i], g.dsp.vals[i]))

    barrier(g)
    with contextlib.ExitStack() as es:
        def sbl(name, shape, dt):
            return es.enter_context(nc.sbuf_tensor(name, list(shape), dt))
        NK = 2 * T // 128
        Kb = [sbl(f"b_K{i}", [96, 2 * T], BF16) for i in range(2)]
        KR = [R() for _ in range(2)]
        KrR = [R() for _ in range(2)]
        Vb = [sbl(f"b_V{i}", [128, NK, 65], BF16) for i in range(2)]
        VR = [R() for _ in range(2)]
        V1R = [R() for _ in range(2)]
        Qb = [sbl(f"b_Q{i}", [96, T], BF16) for i in range(2)]
        QR = [R() for _ in range(2)]
        masks = sbl("b_masks", [128, 4, NT], BF16); masksR = R()
        dma(g.POOL, g.dpl, masks[:, :, :], g.masks_d.rearrange("p (a n) -> p a n", a=4), writes=[masksR])
        Pb = [sbl(f"b_P{i}", [128, NT], BF16) for i in range(3)]
        PR = [R() for _ in range(3)]
        osb = sbl("b_osb", [65, NT], F32); osbR = R()
        rcp = sbl("b_rcp", [65, NT], F32); rcpR = R()
        onesf = sbl("b_onesf", [65, 64], F32); onesfR = R()
        op(g.POOL, lambda: nc.gpsimd.memset(onesf[:], 1.0), writes=[onesfR])
        ob = [sbl(f"b_ob{i}", [64, NT], BF16) for i in range(2)]
        obR = [R() for _ in range(2)]
        scale = float(96 ** -0.5)
        for i in range(2):
            dma(g.SP, g.dsp, Kb[i][64:96, :], g.KrT[:, :], writes=[KrR[i]])
            op(g.POOL, lambda i=i: nc.gpsimd.memset(Vb[i][:, 32:64, 64:65], 1.0), writes=[V1R[i]])
            op(g.DVE, lambda i=i: nc.vector.tensor_copy(out=Vb[i][:, 0:32, 64:65], in_=g.valid[:, 0:1].to_broadcast([128, 32, 1])),
               reads=[g.validR], writes=[V1R[i]])

        def load_head(h):
            b = h % 2
            dma(g.SP, g.dsp, Kb[b][0:64, :], g.KnT[h * 64:(h + 1) * 64, :], writes=[KR[b]])
            dma(g.SP, g.dsp, Qb[b][:, :], g.QT[h, :, :], writes=[QR[b]])
            vsrc = g.Vd.rearrange("(j p) n -> p j n", p=128)
            for q4 in range(4):
                dma(g.SP, g.dsp, Vb[b][:, q4 * 16:(q4 + 1) * 16, 0:64], vsrc[:, q4 * 16:(q4 + 1) * 16, h * 64:(h + 1) * 64],
                    writes=[VR[b]] if q4 == 0 else [], add=[] if q4 == 0 else [VR[b]])

        load_head(0)
        pi = 0
        for h in range(16):
            if h + 1 < 16:
                load_head(h + 1)
            b = h % 2
            for qi in range(8):
                jlist = list(range(32)) + [32 + j for j in range(4 * qi + 4)]
                obank, oR = newbank_long(g)
                prev = None
                nj = len(jlist)

                def emit_S(j):
                    nonlocal pi
                    sbank, sR = newbank(g)
                    begin(g.PE, reads=[KR[b], KrR[b], QR[b]], writes=[sR])
                    ins = nc.tensor.matmul(sbank[:, :], lhsT=Kb[b][:, j * 128:(j + 1) * 128], rhs=Qb[b][:, qi * NT:(qi + 1) * NT], start=True, stop=True)
                    tok = g.PE.mark(ins)
                    end(tok, reads=[KR[b], KrR[b], QR[b]], writes=[sR])
                    P, PR_ = Pb[pi % 3], PR[pi % 3]
                    pi += 1
                    op(g.ACT, lambda: nc.scalar.activation(out=P[:], in_=sbank[:, :], func=AF.Exp, scale=scale), reads=[sR], writes=[PR_])
                    d = j - 32 - 4 * qi
                    if 0 <= d <= 3:
                        op(g.DVE, lambda: nc.vector.tensor_tensor(out=P[:], in0=P[:], in1=masks[:, d, :], op=ALU.mult),
                           reads=[PR_, masksR], writes=[PR_])
                    return P, PR_

                cur = emit_S(jlist[0])
                for idx in range(nj):
                    nxt = emit_S(jlist[idx + 1]) if idx + 1 < nj else None
                    P, PR_ = cur
                    j = jlist[idx]
                    begin(g.PE, reads=[PR_, VR[b], V1R[b]], writes=[oR] if idx == 0 else [])
                    ins = nc.tensor.matmul(obank[0:65, :], lhsT=Vb[b][:, j, :], rhs=P[:], start=(idx == 0), stop=(idx == nj - 1))
                    tok = g.PE.mark(ins)
                    end(tok, reads=[PR_, VR[b], V1R[b]], writes=[oR] if idx == nj - 1 else [])
                    if idx != nj - 1:
                        oR.w = [tok]
                    cur = nxt
                op(g.ACT, lambda: nc.scalar.copy(out=osb[0:64, :], in_=obank[0:64, :]), reads=[oR], writes=[osbR])
                op(g.DVE, lambda: nc.vector.reciprocal(out=rcp[64:65, :], in_=obank[64:65, :]), reads=[oR], writes=[rcpR])
                bb, bbR = newbank(g)
                begin(g.PE, reads=[rcpR, onesfR], writes=[bbR])
                ins = nc.tensor.matmul(bb[0:64, :], lhsT=onesf[64:65, :], rhs=rcp[64:65, :], start=True, stop=True)
                tok = g.PE.mark(ins)
                end(tok, reads=[rcpR, onesfR], writes=[bbR])
                o_, oR_ = ob[(h * 8 + qi) % 2], obR[(h * 8 + qi) % 2]
                op(g.DVE, lambda: nc.vector.tensor_tensor(out=o_[:], in0=bb[0:64, :], in1=osb[0:64, :], op=ALU.mult),
                   reads=[bbR, osbR], writes=[oR_])
                dma(g.SP, g.dsp, g.oT[h * 64:(h + 1) * 64, qi * NT:(qi + 1) * NT], o_[:], reads=[oR_])
        for i in range(len(g.dsp.sems)):
            if g.dsp.vals[i] > 0:
                g.SP.wait((g.dsp.sems[i], g.dsp.vals[i]))

    barrier(g)
    S = alloc_tail(g)
    for t in range(NTILES):
        wplan_extend(g, tail_specs(g))

    def loadC(t):
        load_h_tile(g, S, t)
        load_p_tile(g, S, t)

    zsrc = g.oT.rearrange("(c p) n -> p c n", p=128)
    loadC(0)
    for t in range(NTILES):
        if t + 1 < NTILES:
            loadC(t + 1)
        dma(g.SP, g.dsp, S.z[:, :, :], zsrc[:, :, t * NT:(t + 1) * NT], writes=S.zR)
        tail_tile(g, S, t, S.hT[t % 2], S.hR[t % 2])


def conv_layer(g):
    nc = g.nc
    sb = g.sb
    S = alloc_tail(g)
    HALO = 30
    aT = sb("c_aT", [128, 8, HALO + NT], F32)
    aR = [R() for _ in range(8)]
    acc = sb("c_acc", [128, 8, NT], F32)
    accR = [R() for _ in range(8)]
    accb = S.u
    accbR = S.uR
    mean = sb("c_mean", [128, NT], F32); meanR = R()
    msq = sb("c_msq", [128, NT], F32); msqR = R()
    sg = [sb(f"c_sg{i}", [128, NT], F32) for i in range(2)]
    sgR = [R() for _ in range(2)]

    def specs_pw1():
        sp = []
        for c in range(4):
            sp.append(wspec(g.w_pw1, 8, c * 256, 256))
            sp.append(wspec(g.w_pw1, 8, D + c * 256, 256))
        return sp

    def glu(u, uR, ncols, dst_off, scale_valid=False):
        for c4 in range(4):
            wv, wvR = wget(g)
            wg, wgR = wget(g)
            for cc in range(2):
                c = c4 * 2 + cc
                vb, vR = proj_chunk(g, wv, wvR, cc * 128, u, uR, 8, ncols=ncols)
                gb, gR = proj_chunk(g, wg, wgR, cc * 128, u, uR, 8, ncols=ncols)
                s_, sR_ = sg[c % 2], sgR[c % 2]
                op(g.ACT, lambda gb=gb, s_=s_, c=c: nc.scalar.activation(out=s_[:, 0:ncols], in_=gb[:, 0:ncols], func=AF.Sigmoid,
                                                                        bias=col(g, "b_pw1", 8 + c)),
                   reads=[gR, g.vecsR], writes=[sR_])
                op(g.DVE, lambda vb=vb, s_=s_, c=c: nc.vector.scalar_tensor_tensor(out=aT[:, c, dst_off:dst_off + ncols], in0=vb[:, 0:ncols],
                                                                                scalar=col(g, "b_pw1", c), in1=s_[:, 0:ncols],
                                                                                op0=ALU.add, op1=ALU.mult),
                   reads=[vR, sR_, g.vecsR], writes=[aR[c]])
                if scale_valid:
                    op(g.DVE, lambda c=c: nc.vector.tensor_scalar(out=aT[:, c, dst_off:dst_off + ncols], in0=aT[:, c, dst_off:dst_off + ncols],
                                                                 scalar1=g.valid[:, 0:1], scalar2=None, op0=ALU.mult),
                       reads=[aR[c], g.validR], writes=[aR[c]])

    wplan_extend(g, specs_pw1())
    for t in range(NTILES):
        wplan_extend(g, specs_pw1())
        wplan_extend(g, tail_specs(g))

    HB = 32
    hp = S.hT[1]
    hpR = S.hR[1]
    s = g.hprevT.rearrange("(c p) n -> p c n", p=128)[:, :, T - HB:T]
    dma(g.SP, g.dsp, hp[:, :, 0:HB], s, writes=hpR)
    rmsnorm_tile(g, hp, hpR, "norm1_g", S.u, S.uR, S, ncols=HB)
    glu(S.u, S.uR, HB, HALO + NT - HB, scale_valid=True)
    for c in range(8):
        op(g.POOL, lambda c=c: nc.gpsimd.tensor_copy(out=aT[:, c, 0:HALO], in_=aT[:, c, NT:NT + HALO]), reads=[aR[c]], writes=[aR[c]])

    def loadC(t):
        load_h_tile(g, S, t)
        load_p_tile(g, S, t)

    loadC(0)
    for t in range(NTILES):
        hT, hR = S.hT[t % 2], S.hR[t % 2]
        rmsnorm_tile(g, hT, hR, "norm1_g", S.u, S.uR, S)
        if t + 1 < NTILES:
            loadC(t + 1)
        glu(S.u, S.uR, NT, HALO)
        for c in range(8):
            op(g.DVE, lambda c=c: nc.vector.tensor_scalar(out=acc[:, c, :], in0=aT[:, c, 0:NT], scalar1=col(g, "w_dw", 0 * 8 + c),
                                                         scalar2=col(g, "b_dw", c), op0=ALU.mult, op1=ALU.add),
               reads=[aR[c], g.vecsR], writes=[accR[c]])
            for w in range(1, 31):
                E = g.DVE
                op(E, lambda c=c, w=w: nc.vector.scalar_tensor_tensor(out=acc[:, c, :], in0=aT[:, c, w:w + NT], scalar=col(g, "w_dw", w * 8 + c),
                                                                     in1=acc[:, c, :], op0=ALU.mult, op1=ALU.add),
                   reads=[aR[c], accR[c], g.vecsR], writes=[accR[c]])
            op(g.POOL, lambda c=c: nc.gpsimd.tensor_copy(out=aT[:, c, 0:HALO], in_=aT[:, c, NT:NT + HALO]), reads=[aR[c], accR[c]], writes=[aR[c]])
            op(g.ACT, lambda c=c: nc.scalar.copy(out=accb[:, c, :], in_=acc[:, c, :]), reads=[accR[c]], writes=[accbR[c]])
        dbg_dump(g, "a", t, aT[:, :, HALO:HALO + NT], aR)
        dbg_dump(g, "acc", t, acc[:, :, :], accR)
        op(g.ACT, lambda: nc.scalar.activation(out=S.sq[:, :, :], in_=acc[:, :, :], func=AF.Square), reads=accR, writes=[S.sqR])
        mb, mR = newbank(g)
        begin(g.PE, reads=accbR + [g.onesR], writes=[mR])
        for c in range(8):
            ins = nc.tensor.matmul(mb[:, :], lhsT=g.ones[:], rhs=accb[:, c, :], start=(c == 0), stop=(c == 7))
        tok = g.PE.mark(ins)
        end(tok, reads=accbR + [g.onesR], writes=[mR])
        qb_, qR_ = newbank(g)
        begin(g.PE, reads=[S.sqR, g.onesR], writes=[qR_])
        for c in range(8):
            ins = nc.tensor.matmul(qb_[:, :], lhsT=g.ones[:], rhs=S.sq[:, c, :], start=(c == 0), stop=(c == 7))
        tok = g.PE.mark(ins)
        end(tok, reads=[S.sqR, g.onesR], writes=[qR_])
        op(g.DVE, lambda: nc.vector.tensor_scalar(out=mean[:], in0=mb[:, :], scalar1=1.0 / D, scalar2=None, op0=ALU.mult), reads=[mR], writes=[meanR])
        op(g.DVE, lambda: nc.vector.tensor_tensor(out=msq[:], in0=mean[:], in1=mean[:], op=ALU.mult), reads=[meanR], writes=[msqR])
        op(g.DVE, lambda: nc.vector.scalar_tensor_tensor(out=msq[:], in0=qb_[:, :], scalar=1.0 / D, in1=msq[:], op0=ALU.mult, op1=ALU.subtract),
           reads=[qR_, msqR], writes=[msqR])
        op(g.ACT, lambda: nc.scalar.activation(out=S.rs[:], in_=msq[:], func=AF.Ln, bias=g.epsc[:, 0:1], scale=1.0), reads=[msqR, g.epsR], writes=[S.rsR])
        op(g.ACT, lambda: nc.scalar.activation(out=S.rs[:], in_=S.rs[:], func=AF.Exp, scale=-0.5), reads=[S.rsR], writes=[S.rsR])
        for c in range(8):
            op(g.DVE, lambda c=c: nc.vector.tensor_tensor(out=acc[:, c, :], in0=acc[:, c, :], in1=mean[:], op=ALU.subtract),
               reads=[accR[c], meanR], writes=[accR[c]])
            op(g.DVE, lambda c=c: nc.vector.tensor_tensor(out=acc[:, c, :], in0=acc[:, c, :], in1=S.rs[:], op=ALU.mult),
               reads=[accR[c], S.rsR], writes=[accR[c]])
            op(g.ACT, lambda c=c: nc.scalar.activation(out=S.z[:, c, :], in_=acc[:, c, :], func=AF.Silu, bias=col(g, "ln_b", c), scale=col(g, "ln_g", c)),
               reads=[accR[c], g.vecsR], writes=[S.zR[c]])
        dbg_dump(g, "z", t, S.z[:, :, :], S.zR)
        tail_tile(g, S, t, hT, hR, obias="b_pw2")


def hgrn_layer(g, state_only):
    nc = g.nc
    sb = g.sb
    if state_only:
        S = Ctx()
        S.hT = [sb(f"hT{i}", [128, 8, NT], F32) for i in range(2)]
        S.hR = [[R() for _ in range(8)] for _ in range(2)]
        S.sq = sb("sq", [128, 8, NT], BF16); S.sqR = R()
        S.rs = sb("rs", [128, NT], F32); S.rsR = R()
        S.u = sb("u", [128, 8, NT], BF16); S.uR = [R() for _ in range(8)]
    else:
        S = alloc_tail(g)
    hmask = sb("h_mask", [128, 128], BF16); hmaskR = R()
    dma(g.POOL, g.dpl, hmask[:], g.hmask_d[:, :], writes=[hmaskR])
    reset = sb("h_reset", [128, NT], F32); resetR = R()
    dma(g.SP, g.dsp, reset[:], g.reset_d[:, :], writes=[resetR])
    ident = sb("h_ident", [128, 128], BF16); identR = R()
    dma(g.POOL, g.dpl, ident[:], g.ident_d[:, :], writes=[identR])
    St = sb("h_S", [128, 8, 128], F32)
    SR = [R() for _ in range(8)]
    dma(g.SP, g.dsp, St[:, :, :], g.S0.rearrange("h k v -> k h v"), writes=SR)
    for h in range(8):
        op(g.DVE, lambda h=h: nc.vector.tensor_scalar(out=St[:, h, :], in0=St[:, h, :], scalar1=g.valid[:, 0:1], scalar2=None, op0=ALU.mult),
           reads=[SR[h], g.validR], writes=[SR[h]])
    lb = sb("h_lb", [128, 8], F32); lbR = R()
    oml = sb("h_oml", [128, 8], F32); omlR = R()
    ex = sb("h_ex", [128, 32], F32); exR = R()
    den = sb("h_den", [128, 8], F32); denR = R()
    lo = g.voff["lb_logits"]
    op(g.ACT, lambda: nc.scalar.activation(out=ex[:], in_=g.vecs[:, lo:lo + 32], func=AF.Exp), reads=[g.vecsR], writes=[exR])
    op(g.DVE, lambda: nc.vector.tensor_tensor(out=den[:], in0=ex[:, 0:8], in1=ex[:, 8:16], op=ALU.add), reads=[exR], writes=[denR])
    op(g.DVE, lambda: nc.vector.tensor_tensor(out=den[:], in0=den[:], in1=ex[:, 16:24], op=ALU.add), reads=[exR, denR], writes=[denR])
    op(g.DVE, lambda: nc.vector.tensor_tensor(out=den[:], in0=den[:], in1=ex[:, 24:32], op=ALU.add), reads=[exR, denR], writes=[denR])
    op(g.DVE, lambda: nc.vector.reciprocal(out=den[:], in_=den[:]), reads=[denR], writes=[denR])
    L = g.hg_layer
    op(g.DVE, lambda: nc.vector.tensor_copy(out=lb[:], in_=ex[:, 8:16]), reads=[exR], writes=[lbR])
    for l in range(2, L + 1):
        op(g.DVE, lambda l=l: nc.vector.tensor_tensor(out=lb[:], in0=lb[:], in1=ex[:, 8 * l:8 * l + 8], op=ALU.add), reads=[exR, lbR], writes=[lbR])
    op(g.DVE, lambda: nc.vector.tensor_tensor(out=lb[:], in0=lb[:], in1=den[:], op=ALU.mult), reads=[lbR, denR], writes=[lbR])
    op(g.DVE, lambda: nc.vector.tensor_scalar(out=oml[:], in0=lb[:], scalar1=-1.0, scalar2=1.0, op0=ALU.mult, op1=ALU.add), reads=[lbR], writes=[omlR])

    f_ = [sb(f"h_f{i}", [128, NT], F32) for i in range(2)]; fR = [R() for _ in range(2)]
    lf = [sb(f"h_lf{i}", [128, NT], F32) for i in range(2)]; lfR = [R() for _ in range(2)]
    bb_ = [sb(f"h_b{i}", [128, NT], F32) for i in range(2)]; bR_ = [R() for _ in range(2)]
    e1 = [sb(f"h_e1{i}", [128, NT], F32) for i in range(2)]; e1R = [R() for _ in range(2)]
    e2 = [sb(f"h_e2{i}", [128, NT], F32) for i in range(2)]; e2R = [R() for _ in range(2)]
    dmid = [sb(f"h_dm{i}", [128, 8], F32) for i in range(2)]; dmR = [R() for _ in range(2)]
    Qt = [sb(f"h_Q{i}", [128, NT], BF16) for i in range(2)]; QtR = [R() for _ in range(2)]
    Kt = [sb(f"h_K{i}", [128, NT], BF16) for i in range(2)]; KtR = [R() for _ in range(2)]
    Ktok = [sb(f"h_Kt{i}", [128, 4, 128], BF16) for i in range(2)]; KtokR = [R() for _ in range(2)]
    Vtok = sb("h_V", [128, 4, 1024], BF16); VtokR = [R() for _ in range(4)]
    att = [sb(f"h_att{i}", [128, 128], BF16) for i in range(2)]; attR = [R() for _ in range(2)]
    Sp = [sb(f"h_Sp{i}", [128, 128], BF16) for i in range(2)]; SpR = [R() for _ in range(2)]
    Sd = sb("h_Sd", [128, 128], F32); SdR = R()
    oh = [sb(f"h_o{i}", [128, NT], F32) for i in range(2)]; ohR = [R() for _ in range(2)]
    gsg = [sb(f"h_g{i}", [128, NT], F32) for i in range(2)]; gsgR = [R() for _ in range(2)]
    CH = 64
    NCH = NT // CH
    MID = CH // 2 - 1

    def specs_in():
        sp = []
        if state_only:
            for hf in range(2):
                sp.append(wspec(g.w_hin, 8, 2 * D + hf * 512, 512))
            for h in range(8):
                sp.append(wspec(g.w_hin, 8, D + h * 128, 128))
        else:
            for hf in range(2):
                sp.append(wspec(g.w_hin, 8, 2 * D + hf * 512, 512))
            for h in range(8):
                sp.append(wspec(g.w_hin, 8, h * 128, 128))
                sp.append(wspec(g.w_hin, 8, D + h * 128, 128))
                sp.append(wspec(g.w_hin, 8, 3 * D + h * 128, 128))
        return sp

    for t in range(NTILES):
        wplan_extend(g, specs_in())
        if not state_only:
            wplan_extend(g, tail_specs(g))

    def loadC(t):
        load_h_tile(g, S, t)
        if not state_only:
            load_p_tile(g, S, t)

    def v_proj():
        for hf in range(2):
            wv, wR = wget(g)
            for s4 in range(4):
                bank, bR = newbank(g)
                begin(g.PE, reads=S.uR + [wR], writes=[bR])
                for k in range(8):
                    ins = nc.tensor.matmul(bank[:, :], lhsT=S.u[:, k, s4 * 128:(s4 + 1) * 128], rhs=wv[:, k, :], start=(k == 0), stop=(k == 7))
                tok = g.PE.mark(ins)
                end(tok, reads=S.uR + [wR], writes=[bR])
                op(g.ACT, lambda bank=bank, s4=s4, hf=hf: nc.scalar.copy(out=Vtok[:, s4, hf * 512:(hf + 1) * 512], in_=bank[:, :]),
                   reads=[bR], writes=[VtokR[s4]])

    loadC(0)
    for t in range(NTILES):
        hT, hR = S.hT[t % 2], S.hR[t % 2]
        rmsnorm_tile(g, hT, hR, "norm1_g", S.u, S.uR, S)
        if t + 1 < NTILES:
            loadC(t + 1)
        v_proj()
        for h in range(8):
            p = h % 2
            if not state_only:
                wq, wqR = wget(g)
                qbank, qR = proj_chunk(g, wq, wqR, 0, S.u, S.uR, 8)
            wf, wfR = wget(g)
            fbank, fbR = proj_chunk(g, wf, wfR, 0, S.u, S.uR, 8)
            if not state_only:
                wg_, wgR = wget(g)
                gbank, gbR = proj_chunk(g, wg_, wgR, 0, S.u, S.uR, 8)
                op(g.ACT, lambda: nc.scalar.activation(out=gsg[p][:], in_=gbank[:, :], func=AF.Sigmoid), reads=[gbR], writes=[gsgR[p]])
            op(g.ACT, lambda: nc.scalar.activation(out=f_[p][:], in_=fbank[:, :], func=AF.Sigmoid), reads=[fbR], writes=[fR[p]])
            op(g.DVE, lambda: nc.vector.tensor_scalar(out=f_[p][:], in0=f_[p][:], scalar1=oml[:, h:h + 1], scalar2=lb[:, h:h + 1], op0=ALU.mult, op1=ALU.add),
               reads=[fR[p], omlR, lbR], writes=[fR[p]])
            op(g.ACT, lambda: nc.scalar.activation(out=lf[p][:], in_=f_[p][:], func=AF.Ln), reads=[fR[p]], writes=[lfR[p]])
            op(g.DVE, lambda: nc.vector.tensor_tensor_scan(out=bb_[p][:], data0=reset[:], data1=lf[p][:], initial=0.0, op0=ALU.mult, op1=ALU.add),
               reads=[resetR, lfR[p]], writes=[bR_[p]])
            b3 = bb_[p][:].rearrange("p (c n) -> p c n", n=CH)
            op(g.ACT, lambda: nc.scalar.activation(out=dmid[p][:, :], in_=b3[:, :, MID], func=AF.Exp), reads=[bR_[p]], writes=[dmR[p]])
            op(g.DVE, lambda: nc.vector.tensor_tensor(out=lf[p][:].rearrange("p (c n) -> p c n", n=CH), in0=b3,
                                                      in1=b3[:, :, MID:MID + 1].to_broadcast([128, NCH, CH]), op=ALU.subtract),
               reads=[bR_[p], lfR[p]], writes=[lfR[p]])
            op(g.ACT, lambda: nc.scalar.activation(out=e1[p][:], in_=lf[p][:], func=AF.Exp), reads=[lfR[p]], writes=[e1R[p]])
            op(g.ACT, lambda: nc.scalar.activation(out=e2[p][:], in_=lf[p][:], func=AF.Exp, scale=-1.0), reads=[lfR[p]], writes=[e2R[p]])
            op(g.DVE, lambda: nc.vector.scalar_tensor_tensor(out=Kt[p][:], in0=f_[p][:], scalar=-1.0, in1=e2[p][:], op0=ALU.add, op1=ALU.mult),
               reads=[fR[p], e2R[p]], writes=[KtR[p]])
            if not state_only:
                op(g.DVE, lambda: nc.vector.tensor_tensor(out=Qt[p][:], in0=qbank[:, :], in1=e1[p][:], op=ALU.mult),
                   reads=[qR, e1R[p]], writes=[QtR[p]])
            for s4 in range(4):
                tb, tR = newbank(g)
                tview = tb[:, 0:64].bitcast(BF16)
                begin(g.PE, reads=[KtR[p], identR], writes=[tR])
                ins = nc.tensor.transpose(tview, Kt[p][:, s4 * 128:(s4 + 1) * 128], ident[:])
                tok = g.PE.mark(ins)
                end(tok, reads=[KtR[p], identR], writes=[tR])
                op(g.ACT, lambda s4=s4, tview=tview: nc.scalar.copy(out=Ktok[p][:, s4, :], in_=tview), reads=[tR], writes=[KtokR[p]])
            e13 = e1[p][:].rearrange("p (c n) -> p c n", n=CH)
            for s4 in range(4):
                if not state_only:
                    ab, aR_ = newbank(g)
                    begin(g.PE, reads=[KtR[p], QtR[p]], writes=[aR_])
                    ins = nc.tensor.matmul(ab[:, 0:128], lhsT=Kt[p][:, s4 * 128:(s4 + 1) * 128], rhs=Qt[p][:, s4 * 128:(s4 + 1) * 128], start=True, stop=True)
                    tok = g.PE.mark(ins)
                    end(tok, reads=[KtR[p], QtR[p]], writes=[aR_])
                    at, atR = att[s4 % 2], attR[s4 % 2]
                    op(g.DVE, lambda ab=ab, at=at: nc.vector.tensor_tensor(out=at[:], in0=ab[:, 0:128], in1=hmask[:], op=ALU.mult),
                       reads=[aR_, hmaskR], writes=[atR])
                    obank, obR = newbank_long(g)
                    begin(g.PE, reads=[VtokR[s4], atR], writes=[obR])
                    ins = nc.tensor.matmul(obank[:, 0:128], lhsT=Vtok[:, s4, h * 128:(h + 1) * 128], rhs=at[:], start=True, stop=False)
                    tok = g.PE.mark(ins)
                    end(tok, reads=[VtokR[s4], atR], writes=[])
                    obR.w = [tok]
                for cc in range(2):
                    ci = s4 * 2 + cc
                    lo_ = cc * 64
                    if not state_only:
                        sp_, spR_ = Sp[ci % 2], SpR[ci % 2]
                        op(g.ACT, lambda sp_=sp_, ci=ci: nc.scalar.activation(out=sp_[:], in_=St[:, h, :], func=AF.Copy, scale=dmid[p][:, ci:ci + 1]),
                           reads=[SR[h], dmR[p]], writes=[spR_])
                        begin(g.PE, reads=[spR_, QtR[p]], writes=[])
                        ins = nc.tensor.matmul(obank[:, lo_:lo_ + 64], lhsT=sp_[:], rhs=Qt[p][:, ci * 64:(ci + 1) * 64], start=False, stop=(cc == 1),
                                               skip_group_check=True)
                        tok = g.PE.mark(ins)
                        end(tok, reads=[spR_, QtR[p]], writes=[])
                        obR.w = [tok]
                    kb_, kbR = newbank(g)
                    begin(g.PE, reads=[KtokR[p], VtokR[s4]], writes=[kbR])
                    ins = nc.tensor.matmul(kb_[:, 0:128], lhsT=Ktok[p][lo_:lo_ + 64, s4, :], rhs=Vtok[lo_:lo_ + 64, s4, h * 128:(h + 1) * 128], start=True, stop=True)
                    tok = g.PE.mark(ins)
                    end(tok, reads=[KtokR[p], VtokR[s4]], writes=[kbR])
                    op(g.DVE, lambda kb_=kb_, ci=ci: nc.vector.scalar_tensor_tensor(out=Sd[:], in0=St[:, h, :], scalar=dmid[p][:, ci:ci + 1], in1=kb_[:, 0:128],
                                                                             op0=ALU.mult, op1=ALU.subtract),
                       reads=[SR[h], dmR[p], kbR], writes=[SdR])
                    op(g.DVE, lambda ci=ci: nc.vector.tensor_scalar(out=St[:, h, :], in0=Sd[:], scalar1=e13[:, ci, CH - 1:CH], scalar2=None, op0=ALU.mult),
                       reads=[SdR, e1R[p]], writes=[SR[h]])
                if not state_only:
                    op(g.ACT, lambda obank=obank, s4=s4: nc.scalar.copy(out=oh[p][:, s4 * 128:(s4 + 1) * 128], in_=obank[:, 0:128]), reads=[obR], writes=[ohR[p]])
            if not state_only:
                op(g.ACT, lambda: nc.scalar.activation(out=S.sq[:, 0, :], in_=oh[p][:], func=AF.Square), reads=[ohR[p]], writes=[S.sqR])
                nb, nR = newbank(g)
                begin(g.PE, reads=[S.sqR, g.onesR], writes=[nR])
                ins = nc.tensor.matmul(nb[:, :], lhsT=g.ones[:], rhs=S.sq[:, 0, :], start=True, stop=True)
                tok = g.PE.mark(ins)
                end(tok, reads=[S.sqR, g.onesR], writes=[nR])
                rstd_from_sumsq(g, nb, nR, 128, S.rs, S.rsR)
                op(g.DVE, lambda: nc.vector.scalar_tensor_tensor(out=oh[p][:], in0=oh[p][:], scalar=col(g, "hnorm_g", h), in1=S.rs[:], op0=ALU.mult, op1=ALU.mult),
                   reads=[ohR[p], S.rsR, g.vecsR], writes=[ohR[p]])
                op(g.DVE, lambda: nc.vector.tensor_tensor(out=S.z[:, h, :], in0=oh[p][:], in1=gsg[p][:], op=ALU.mult),
                   reads=[ohR[p], gsgR[p]], writes=[S.zR[h]])
        if not state_only:
            tail_tile(g, S, t, hT, hR)
    dma(g.SP, g.dsp, g.Sout.rearrange("h k v -> k h v"), St[:, :, :], reads=SR)


def _consts():
    c = {}
    p = np.arange(128)[:, None]
    f = np.arange(NT)[None, :]
    m = np.zeros((128, 4, NT), np.float32)
    for d in range(4):
        m[:, d, :] = (p + d * 128 <= f).astype(np.float32)
    c["masks"] = np.ascontiguousarray(m.reshape(128, 4 * NT))
    s = np.arange(128)[:, None]
    t = np.arange(128)[None, :]
    c["hmask"] = (-1.0 * ((s <= t) & (s // 64 == t // 64))).astype(np.float32)
    r = np.ones((128, NT), np.float32)
    r[:, ::64] = 0.0
    c["reset"] = r
    c["ident"] = np.eye(128, dtype=np.float32)
    return c


def _inv_freq_col():
    inv = (1.0 / (np.float32(10000.0) ** (np.arange(0, 32, 2, dtype=np.float32) / np.float32(32)))).astype(np.float32)
    colv = np.zeros((128, 1), np.float32)
    colv[0:32, 0] = np.concatenate([inv, inv])
    return colv


_PROG_CACHE = {}


def _get_prog(key, *args, **kw):
    if key not in _PROG_CACHE:
        hg_layer = kw.pop("hg_layer", None)
        if hg_layer is not None:
            global _HG_LAYER
            _HG_LAYER = hg_layer
        _PROG_CACHE[key] = build_program(*args, **kw)
    return _PROG_CACHE[key]


def run_layer(i, inp, hT_list, hprev_list, S0_list=None, state_only=False):
    kind = ("mla", "conv", "hgrn")[i % 3]
    j = i // 3
    last = (i == 3)
    C = _consts()
    vp = VecPack()
    vp.add("norm1_g", fm(inp["norm1_g"][i]))
    vp.add("norm2_g", fm(inp["norm2_g"][i]))
    vp.add("ple_norm_g", fm(inp["ple_norm_g"][i]))
    vp.add("final_g", fm(inp["final_norm_g"]))
    common = {}
    if kind == "mla":
        vp.add("q_norm_g", fm(inp["mla_q_norm_g"][j]))
        vp.add("kv_norm_g", fm(inp["mla_kv_norm_g"][j]))
        vp.add("inv_freq", _inv_freq_col())
        w_in = inp["mla_w_in"][j]
        kr = w_in[:, 640:672]
        common["w_in"] = np.ascontiguousarray(np.concatenate([w_in, kr[:, 16:32], kr[:, 0:16]], axis=1))
        wq = inp["mla_w_uq"][j].reshape(384, 16, 96)
        common["w_uq"] = np.ascontiguousarray(wq.reshape(384, 1536))
        common["w_uqs"] = np.ascontiguousarray(np.concatenate([wq[:, :, 80:96], wq[:, :, 64:80]], axis=2).reshape(384, 512))
        wkv = inp["mla_w_ukv"][j].reshape(256, 16, 128)
        common["w_ukv"] = np.ascontiguousarray(np.concatenate([wkv[:, :, 0:64].reshape(256, 1024), wkv[:, :, 64:128].reshape(256, 1024)], axis=1))
        common["masks"] = C["masks"]
        common["w_o"] = inp["mla_w_out"][j]
    elif kind == "conv":
        vp.add("b_pw1", fm(inp["conv_b_pw1"][j]))
        vp.add("w_dw", np.concatenate([fm(inp["conv_w_dw"][j][w]) for w in range(31)], axis=1))
        vp.add("b_dw", fm(inp["conv_b_dw"][j]))
        vp.add("ln_g", fm(inp["conv_ln_g"][j]))
        vp.add("ln_b", fm(inp["conv_ln_b"][j]))
        vp.add("b_pw2", fm(inp["conv_b_pw2"][j]))
        common["w_pw1"] = inp["conv_w_pw1"][j]
        common["w_o"] = inp["conv_w_pw2"][j]
    else:
        vp.add("lb_logits", np.concatenate([fm(inp["hgrn_lb_logits"][l]) for l in range(4)], axis=1))
        vp.add("hnorm_g", fm(inp["hgrn_norm_g"][j]))
        common["w_hin"] = inp["hgrn_w_in"][j]
        common["hmask"] = C["hmask"]
        common["reset"] = C["reset"]
        common["ident"] = C["ident"]
        if not state_only:
            common["w_o"] = inp["hgrn_w_out"][j]
    vecs = vp.build()
    common["vecs"] = vecs
    if not state_only:
        common["w_gu"] = inp["ffn_w_gu"][i]
        common["w_down"] = inp["ffn_w_down"][i]
        common["w_pproj"] = inp["ple_w_proj"][i]
        common["w_pgate"] = inp["ple_w_gate"][i]
    common = {k: np.ascontiguousarray(v, dtype=v.dtype if v.dtype == np.int32 else np.float32) for k, v in common.items()}

    global _HG_LAYER
    key = (kind, last, state_only, i if kind == "hgrn" else -1, vecs.shape[1])
    if key not in _PROG_CACHE:
        _HG_LAYER = i
        _PROG_CACHE[key] = _build(i, kind, vp.off, vecs.shape[1], last, state_only)
    nc = _PROG_CACHE[key]

    in_maps = []
    for core in range(NCORES):
        b, half = core // 2, core % 2
        m = dict(common)
        m["hT"] = hT_list[core]
        m["hprevT"] = hprev_list[core]
        m["valid"] = np.full((128, 1), float(half), np.float32)
        if not state_only:
            m["pT"] = np.ascontiguousarray(inp["p"][i, b, half * T:(half + 1) * T, :].T)
        if kind == "mla":
            pos = inp["positions"][b]
            own = pos[half * T:(half + 1) * T]
            prev = pos[0:T]
            m["pos"] = np.ascontiguousarray(np.concatenate([prev, own])[None, :].astype(np.int32))
        if kind == "hgrn":
            m["S0"] = S0_list[core]
        in_maps.append(m)
    res = run_bass_kernel_spmd(nc, in_maps, core_ids=list(range(NCORES)))
    return res.results


def _build(i, kind, voff, nvec, last, state_only):
    orig = build_program

    def patched():
        return None
    Ctx.hg_layer = i
    return build_program(i, kind, voff, nvec, last, state_only)


def kernel(**inp):
    inp = {k: np.asarray(v) for k, v in inp.items()}
    x = inp["x"]
    hT = []
    for core in range(NCORES):
        b, half = core // 2, core % 2
        hT.append(np.ascontiguousarray(x[b, half * T:(half + 1) * T, :].T))

    def prevs(hT):
        return [hT[(core // 2) * 2] for core in range(NCORES)]

    for i in range(4):
        kind = ("mla", "conv", "hgrn")[i % 3]
        if kind == "hgrn":
            zeros = [np.zeros((8, 128, 128), np.float32) for _ in range(NCORES)]
            r1 = run_layer(i, inp, hT, prevs(hT), S0_list=zeros, state_only=True)
            S0 = [r1[(core // 2) * 2]["Sout"] for core in range(NCORES)]
            r = run_layer(i, inp, hT, prevs(hT), S0_list=S0, state_only=False)
        else:
            r = run_layer(i, inp, hT, prevs(hT))
        hT = [np.ascontiguousarray(r[core]["outT"]) for core in range(NCORES)]
    out = np.empty_like(x)
    for core in range(NCORES):
        b, half = core // 2, core % 2
        out[b, half * T:(half + 1) * T, :] = hT[core].T
    return out
```
